# Optimizing a Trainium2 kernel written in Bass

```python
import jax, jax.numpy as jnp
from jax import lax
import numpy as np

D_MODEL = 4096
BATCH = 1
SEQ = 8192
DEPTH = 1

EPS = 1e-6
POOL_WIDTH = D_MODEL // 2
POOL_WINDOWS = (2, 4, 8, 16)
N_POOL_GROUPS = len(POOL_WINDOWS)
POOL_GROUP = POOL_WIDTH // N_POOL_GROUPS
HGRN_WIDTH = D_MODEL // 2
HGRN_HEAD_DIM = 128
HGRN_HEADS = HGRN_WIDTH // HGRN_HEAD_DIM
CHUNK = 64
D_FF = -(-8 * D_MODEL // (3 * 256)) * 256
N_IN = POOL_WIDTH + 4 * HGRN_WIDTH + 2 * D_MODEL
SPLITS = (POOL_WIDTH,
          POOL_WIDTH + HGRN_WIDTH,
          POOL_WIDTH + 2 * HGRN_WIDTH,
          POOL_WIDTH + 3 * HGRN_WIDTH,
          POOL_WIDTH + 4 * HGRN_WIDTH,
          POOL_WIDTH + 4 * HGRN_WIDTH + D_MODEL)

kernel_name = "hybrid_pool_hgrn2_gated_block"


def rms_norm(x, w):
    xf = x.astype(jnp.float32)
    y = xf * lax.rsqrt(jnp.mean(xf * xf, axis=-1, keepdims=True) + EPS)
    return (y * w.astype(jnp.float32)).astype(x.dtype)


def causal_multiscale_pool(z):
    B, T, _ = z.shape
    zf = z.astype(jnp.float32).reshape(B, T, N_POOL_GROUPS, POOL_GROUP)
    cs = jnp.concatenate([jnp.zeros((B, 1, N_POOL_GROUPS, POOL_GROUP), jnp.float32),
                          jnp.cumsum(zf, axis=1)], axis=1)
    t = jnp.arange(T)
    outs = []
    for gi, w in enumerate(POOL_WINDOWS):
        start = jnp.maximum(t + 1 - w, 0)
        window_sum = cs[:, 1:, gi] - cs[:, start, gi]
        count = (t + 1 - start).astype(jnp.float32)[None, :, None]
        outs.append(window_sum / count)
    pooled = jnp.stack(outs, axis=2)
    return pooled - zf


def hgrn2_chunkwise(q, k, v, log_f):
    B, T, H, DK = q.shape
    DV = v.shape[-1]
    n = T // CHUNK

    def to_chunks(a):
        return a.reshape(B, n, CHUNK, H, a.shape[-1]).transpose(1, 0, 3, 2, 4)

    qc, kc, vc, gc = to_chunks(q), to_chunks(k), to_chunks(v), to_chunks(log_f)
    causal = jnp.tril(jnp.ones((CHUNK, CHUNK), dtype=bool))[:, :, None]

    def step(S, inp):
        qb, kb, vb, gb = inp
        b = jnp.cumsum(gb, axis=2)
        diff = b[:, :, :, None, :] - b[:, :, None, :, :]
        decay = jnp.exp(jnp.where(causal, diff, -jnp.inf))
        scores = jnp.einsum('bhtk,bhtsk,bhsk->bhts', qb, decay, kb)
        o_intra = jnp.einsum('bhts,bhsv->bhtv', scores, vb)
        o_inter = jnp.einsum('bhtk,bhkv->bhtv', qb * jnp.exp(b), S)
        b_end = b[:, :, -1:, :]
        k_dec = kb * jnp.exp(b_end - b)
        S_new = jnp.exp(b_end[:, :, 0, :])[..., None] * S + jnp.einsum('bhsk,bhsv->bhkv', k_dec, vb)
        return S_new, o_intra + o_inter

    S0 = jnp.zeros((B, H, DK, DV), jnp.float32)
    _, ys = lax.scan(step, S0, (qc, kc, vc, gc))
    return ys.transpose(1, 0, 3, 2, 4).reshape(B, T, H, DV)


def setup_inputs(seed: int = 0) -> dict:
    key = jax.random.key(seed)
    ks = jax.random.split(key, 16)
    f32 = jnp.float32

    def dense(k, shape, fan_in):
        return jax.random.normal(k, shape, f32) * (fan_in ** -0.5)

    def gain(k, shape):
        return 1.0 + 0.02 * jax.random.normal(k, shape, f32)

    L = DEPTH
    return {
        "x": jax.random.normal(ks[0], (BATCH, SEQ, D_MODEL), f32),
        "g_mix": gain(ks[1], (L, D_MODEL)),
        "w_in": dense(ks[2], (L, D_MODEL, N_IN), D_MODEL),
        "w_pool_group": dense(ks[3], (L, N_POOL_GROUPS, POOL_GROUP, POOL_GROUP), POOL_GROUP),
        "pool_scale": gain(ks[4], (L, POOL_WIDTH)),
        "lb_param": 0.5 * jax.random.normal(ks[5], (L + 1, HGRN_WIDTH), f32),
        "hgrn_norm": gain(ks[6], (L, HGRN_WIDTH)),
        "w_up_pool": dense(ks[7], (L, POOL_WIDTH, D_MODEL), POOL_WIDTH),
        "w_up_hgrn": dense(ks[8], (L, HGRN_WIDTH, D_MODEL), HGRN_WIDTH),
        "w_out": dense(ks[9], (L, D_MODEL, D_MODEL), D_MODEL),
        "g_ffn": gain(ks[10], (L, D_MODEL)),
        "w_ffn_gate": dense(ks[11], (L, D_MODEL, D_FF), D_MODEL),
        "w_ffn_up": dense(ks[12], (L, D_MODEL, D_FF), D_MODEL),
        "w_ffn_down": dense(ks[13], (L, D_FF, D_MODEL), D_FF),
        "g_final": gain(ks[14], (D_MODEL,)),
    }


def reference(x, g_mix, w_in, w_pool_group, pool_scale, lb_param, hgrn_norm,
              w_up_pool, w_up_hgrn, w_out, g_ffn, w_ffn_gate, w_ffn_up, w_ffn_down, g_final):
    B, T, _ = x.shape
    f32 = jnp.float32
    lower_bounds = jnp.cumsum(jax.nn.softmax(lb_param.astype(f32), axis=0), axis=0)

    def heads(a):
        return a.reshape(B, T, HGRN_HEADS, HGRN_HEAD_DIM)

    h = x
    for l in range(DEPTH):
        u = rms_norm(h, g_mix[l])
        proj = u @ w_in[l]
        z_pool, q, f_logit, i_in, o_gate, gate_a, gate_b = jnp.split(proj, SPLITS, axis=-1)

        pooled = causal_multiscale_pool(z_pool).astype(x.dtype)
        y_pool = jnp.einsum('btng,nge->btne', pooled, w_pool_group[l])
        y_pool = (y_pool * pool_scale[l].reshape(N_POOL_GROUPS, POOL_GROUP)).reshape(B, T, POOL_WIDTH)

        lb = lower_bounds[l]
        f = lb + (1.0 - lb) * jax.nn.sigmoid(f_logit.astype(f32))
        k = 1.0 - f
        log_f = jnp.log(f)
        qh = jax.nn.silu(q.astype(f32))
        o = hgrn2_chunkwise(heads(qh), heads(k), heads(i_in.astype(f32)), heads(log_f))
        o = rms_norm(o, hgrn_norm[l].reshape(HGRN_HEADS, HGRN_HEAD_DIM))
        y_hgrn = (o.reshape(B, T, HGRN_WIDTH) * jax.nn.silu(o_gate.astype(f32))).astype(x.dtype)

        merged = (jax.nn.sigmoid(gate_a) * (y_pool @ w_up_pool[l])
                  + jax.nn.sigmoid(gate_b) * (y_hgrn @ w_up_hgrn[l]))
        h = h + merged @ w_out[l]

        v = rms_norm(h, g_ffn[l])
        h = h + (jax.nn.silu(v @ w_ffn_gate[l]) * (v @ w_ffn_up[l])) @ w_ffn_down[l]

    return rms_norm(h, g_final)
```

```python
import numpy as np
import concourse.bass as bass
import concourse.mybir as mybir
from concourse.bass_utils import run_bass_kernel_spmd

F32 = mybir.dt.float32
BF16 = mybir.dt.bfloat16
AF = mybir.ActivationFunctionType
ALU = mybir.AluOpType
EPS = 1e-6
HALO = 16
CH = 64


class Cfg:
    def __init__(self, D=4096, SEQ=8192, NCORES=8, TT=512):
        self.D = D
        self.SEQ = SEQ
        self.NCORES = NCORES
        self.TC = SEQ // NCORES
        self.TT = TT
        self.NT = self.TC // TT
        self.KC = D // 128
        self.PW = D // 2
        self.PG = self.PW // 4
        self.PGC = self.PG // 128
        self.HW = D // 2
        self.H = self.HW // 128
        self.DFF = -(-8 * D // (3 * 256)) * 256
        self.FC = self.DFF // 128
        self.NIN = self.PW + 4 * self.HW + 2 * D
        self.OFF_Q = self.PW
        self.OFF_F = self.PW + self.HW
        self.OFF_I = self.PW + 2 * self.HW
        self.OFF_OG = self.PW + 3 * self.HW
        self.OFF_GA = self.PW + 4 * self.HW
        self.OFF_GB = self.PW + 4 * self.HW + D
        self.NCH = TT // CH
        self.W = HALO + TT


ENGS = ("pe", "act", "dve", "pool", "sp")


class Prog:
    def __init__(self, nc):
        self.nc = nc
        self.q = {e: [] for e in ENGS}
        self.cnt = {}
        self.sem = {}
        self.seen = {e: {} for e in ENGS}
        self.lastw = {}
        self.readers = {}
        self._stack = None

    def add_sem(self, name, handle):
        self.sem[name] = handle
        self.cnt[name] = 0

    def _need(self, E, deps):
        for tgt, c in deps.items():
            if c > self.seen[E].get(tgt, 0):
                self.seen[E][tgt] = c
                sem = self.sem[tgt]
                self.q[E].append(lambda eng, sem=sem, c=c: eng.wait_ge(sem, c))

    def _deps(self, E, reads, writes):
        deps = {}

        def need(t, c):
            if t == "pe" and E == "pe":
                return
            if c > deps.get(t, 0):
                deps[t] = c

        for r in reads:
            lw = self.lastw.get(r)
            if lw:
                need(*lw)
        for w in writes:
            lw = self.lastw.get(w)
            if lw:
                need(*lw)
            for t, c in self.readers.get(w, {}).items():
                need(t, c)
        return deps

    def _commit(self, who, c, reads, writes):
        for r in reads:
            d = self.readers.setdefault(r, {})
            if c > d.get(who, 0):
                d[who] = c
        for w in writes:
            self.lastw[w] = (who, c)
            self.readers[w] = {}

    def op(self, E, fn, reads=(), writes=()):
        psk = [r for r in reads if isinstance(r, tuple) and r[0] == "ps"]
        if psk:
            reads = [r for r in reads if not (isinstance(r, tuple) and r[0] == "ps")]
            writes = list(writes) + [k for k in psk if k not in writes]
        self._need(E, self._deps(E, reads, writes))
        self.cnt[E] += 1
        c = self.cnt[E]
        sem = self.sem[E]
        self.q[E].append(lambda eng, fn=fn, sem=sem: fn(eng).then_inc(sem, 1))
        self._commit(E, c, reads, writes)
        return c

    def dma(self, Q, fn, reads, writes, semname, inc=16, extra_wait=None):
        deps = self._deps(Q, reads, writes)
        if extra_wait:
            for t, c in extra_wait:
                if c > deps.get(t, 0):
                    deps[t] = c
        self._need(Q, deps)
        self.cnt[semname] += inc
        c = self.cnt[semname]
        sem = self.sem[semname]
        self.q[Q].append(lambda eng, fn=fn, sem=sem, inc=inc: fn(eng).then_inc(sem, inc))
        self._commit(semname, c, reads, writes)
        return (semname, c)

    def barrier(self, soft=False):
        for E in ENGS:
            if soft and E == "pool":
                continue
            deps = {t: c for t, c in self.cnt.items() if t != E and c > 0
                    and not (soft and (t == "pool" or t.startswith("d:w")))}
            self._need(E, deps)
        if soft:
            self.lastw = {k: v for k, v in self.lastw.items() if isinstance(k, tuple) and k[0] == "w"}
            self.readers = {k: v for k, v in self.readers.items() if isinstance(k, tuple) and k[0] == "w"}
        else:
            self.lastw = {}
            self.readers = {}

    def final_wait(self, E, names):
        self._need(E, {t: self.cnt[t] for t in names if self.cnt[t] > 0})


class Arena:
    def __init__(self, tens, ncols):
        self.t = tens
        self.n = ncols
        self.off = 0

    def take(self, ncols, dtype=BF16, shape=None):
        ncols_bf = ncols * (2 if dtype == F32 else 1)
        if self.off % 2:
            self.off += 1
        o = self.off
        self.off += ncols_bf
        assert self.off <= self.n, f"arena overflow {self.off} > {self.n}"
        ap = self.t[:, o:o + ncols_bf]
        if dtype == F32:
            ap = ap.bitcast(F32)
        return ap

    def mark(self):
        return self.off

    def reset(self, m):
        self.off = m


def v3(ap, a):
    return ap.rearrange("p (a b) -> p a b", a=a)


DEBUG = False
WINC = 16
STOP = None


def build_program(cfg):
    c = cfg
    D, KC, TT, W, H, NCH, NT, TC = c.D, c.KC, c.TT, c.W, c.H, c.NCH, c.NT, c.TC
    nc = bass.Bass("TRN2", target_bir_lowering=False)

    def dram(name, shape, dt=F32, kind="ExternalInput"):
        return nc.dram_tensor(name, list(shape), dt, kind=kind).ap()

    xT = dram("xT", [D, HALO + TC])
    w_in = dram("w_in", [D, c.NIN])
    w_pool = dram("w_pool", [4 * c.PG, c.PG])
    w_up_pool = dram("w_up_pool", [c.PW, D])
    w_up_hgrn = dram("w_up_hgrn", [c.HW, D])
    w_out = dram("w_out", [D, D])
    w_gate = dram("w_gate", [D, c.DFF])
    w_up = dram("w_up", [D, c.DFF])
    w_down = dram("w_down", [c.DFF, D])
    g_mix = dram("g_mix", [128, KC])
    g_ffn = dram("g_ffn", [128, KC])
    g_final = dram("g_final", [128, KC])
    pool_scale = dram("pool_scale", [128, 4 * c.PGC])
    hgrn_norm = dram("hgrn_norm", [128, H])
    lbp = dram("lbp", [128, 2 * H])
    ident_d = dram("ident", [128, 128])
    cmask_d = dram("cmask", [CH, CH])
    rmask_d = dram("rmask", [128, 2 * c.NCORES])
    rstm_d = dram("rstm", [128, TT])
    pinv_d = dram("pinv", [NT * 128, 4 * HALO])
    outT = dram("outT", [D, TC], kind="ExternalOutput")
    SW = H * 128 + H
    cc_src = dram("cc_src", [128, SW], kind="Internal")
    cc_dst = dram("cc_dst", [c.NCORES * 128, SW], kind="Internal")
    onorm_d = dram("onorm_d", [H * 128, TC], BF16, kind="Internal")
    if DEBUG:
        dbg = dram("dbg", [2 * (KC // 2) * 128, TC], BF16, kind="ExternalOutput")

    NB = 98 * 1024
    NSLOT = 4
    SLOT = 4096
    import contextlib
    with contextlib.ExitStack() as es:
        arena_t = es.enter_context(nc.sbuf_tensor("arena", [128, NB], BF16))
        psb = [es.enter_context(nc.psum_tensor(f"ps{i}", [128, 512], F32)) for i in range(8)]
        P = Prog(nc)
        for e in ENGS:
            P.add_sem(e, es.enter_context(nc.semaphore("s_" + e)))
        dsems = [f"d:w{i}" for i in range(NSLOT)] + ["d:x0", "d:x1", "d:x2", "d:x3", "d:x4", "d:x5", "d:c", "d:cc", "d:o0", "d:o1", "d:m", "d:r0", "d:r1", "d:on0", "d:on1", "d:pv"]
        for s in dsems:
            P.add_sem(s, es.enter_context(nc.semaphore("s_" + s.replace(":", "_"))))
        block = es.enter_context(nc.Block())

        A = Arena(arena_t, NB)
        ident = A.take(128)
        ones = A.take(128)
        cmask = A.take(CH, F32)
        rstm = A.take(TT, F32)
        gmix_s = A.take(KC, F32)
        gffn_s = A.take(KC, F32)
        gfin_s = A.take(KC, F32)
        pscale_s = A.take(4 * c.PGC, F32)
        hnorm_s = A.take(H, F32)
        lb_s = A.take(H, F32)
        oml_s = A.take(H, F32)
        noml_s = A.take(H, F32)
        rmask_s = A.take(2 * c.NCORES, F32)
        lbp_s = A.take(2 * H, F32)
        lbt = A.take(2 * H, F32)
        wring = [A.take(SLOT) for _ in range(NSLOT)]
        NXS = 6
        xs = [A.take(W, F32) for _ in range(NXS)]
        sqb = [A.take(W) for _ in range(4)]
        rstd_b = A.take(W, F32)
        rstd_c = A.take(W, F32)
        base_mark = A.mark()

        state = {"wl": 0, "wl_hist": [], "xl": 0, "psr": 0, "nrot": 3}

        def wload(dram3, kcn, ncols):
            i = state["wl"]
            state["wl"] += 1
            s = i % NSLOT
            dst = v3(wring[s][:, 0:kcn * ncols], kcn)
            extra = None
            if len(state["wl_hist"]) >= 3:
                extra = [state["wl_hist"][-3]]
            tok = P.dma("pool", lambda g, dst=dst, src=dram3: g.dma_start(out=dst, in_=src),
                        reads=(), writes=[("w", s)], semname=f"d:w{s}", extra_wait=extra, inc=WINC)
            state["wl_hist"].append(tok)
            return ("w", s), dst

        def wview(wd, rows0, nrows, c0, ncols):
            return wd[rows0:rows0 + nrows, c0:c0 + ncols].rearrange("(kc p) n -> p kc n", p=128)

        def next_ps():
            i = state["psr"] % state["nrot"]
            state["psr"] += 1
            return ("ps", i), psb[i]

        def mm(ps_key, ps_ap, pairs, reads):
            def fn(pe, pairs=pairs, ps_ap=ps_ap):
                n = len(pairs)
                ins = None
                for i, (l, r) in enumerate(pairs):
                    ins = pe.matmul(ps_ap, l, r, start=(i == 0), stop=(i == n - 1))
                return ins
            P.op("pe", fn, reads=reads, writes=[ps_key])

        def act(out, in_, func, reads, writes, scale=None, bias=None):
            kw = {}
            if scale is not None:
                kw["scale"] = scale
            if bias is not None:
                kw["bias"] = bias
            P.op("act", lambda e: e.activation(out, in_, func, **kw), reads=reads, writes=writes)

        def tt(out, a, b, op, reads, writes, eng="dve"):
            P.op(eng, lambda e: e.tensor_tensor(out, a, b, op), reads=reads, writes=writes)

        def ts(out, a, s1, s2, op0, op1, reads, writes):
            P.op("dve", lambda e: e.tensor_scalar(out, a, s1, s2, op0, op1), reads=reads, writes=writes)

        def stt(out, a, s, b, op0, op1, reads, writes):
            P.op("dve", lambda e: e.scalar_tensor_tensor(out, a, s, b, op0, op1), reads=reads, writes=writes)

        def cp(out, in_, reads, writes, eng="dve"):
            if eng == "act":
                P.op(eng, lambda e: e.copy(out, in_), reads=reads, writes=writes)
            else:
                P.op(eng, lambda e: e.tensor_copy(out, in_), reads=reads, writes=writes)

        def xload(rows0, col0, ncols, alt=False):
            i = state["xl"]
            state["xl"] += 1
            s = i % NXS
            dst = xs[s][:, 0:ncols]
            src = xT[rows0:rows0 + 128, col0:col0 + ncols]
            q = "pool" if (alt and i % 2) else "sp"
            P.dma(q, lambda g: g.dma_start(out=dst, in_=src), reads=(), writes=[("xs", s)], semname=f"d:x{s}")
            return ("xs", s), dst

        def small_load(dst, src, key, q="sp", cast=False):
            P.dma("pool" if cast else q, lambda g: g.dma_start(out=dst, in_=src), reads=(), writes=[key], semname="d:c")

        small_load(ident, ident_d, "ident", cast=True)
        small_load(cmask[0:CH, :], cmask_d, "cmask")
        small_load(rstm, rstm_d, "rstm")
        small_load(gmix_s, g_mix, "gmix")
        small_load(gffn_s, g_ffn, "gffn")
        small_load(gfin_s, g_final, "gfin")
        small_load(pscale_s, pool_scale, "pscale")
        small_load(hnorm_s, hgrn_norm, "hnorm")
        small_load(lbp_s, lbp, "lbp")
        small_load(rmask_s, rmask_d, "rmask")
        P.op("dve", lambda e: e.memset(ones, 1.0), writes=["ones"])
        lb3 = v3(lbp_s, 2)
        lbt3 = v3(lbt, 2)
        tt(lbt3[:, 0, :], lb3[:, 0, :], lb3[:, 1, :], ALU.max, ["lbp"], ["lbmx"])
        tt(lbt3[:, 1, :], lb3[:, 1, :], lbt3[:, 0, :], ALU.subtract, ["lbp", "lbmx"], ["lbd1"])
        tt(lbt3[:, 0, :], lb3[:, 0, :], lbt3[:, 0, :], ALU.subtract, ["lbp", "lbmx", "lbd1"], ["lbd0"])
        act(lbt, lbt, AF.Exp, ["lbd0", "lbd1"], ["lbe"])
        tt(oml_s, lbt3[:, 0, :], lbt3[:, 1, :], ALU.add, ["lbe"], ["lbsum"])
        P.op("dve", lambda e: e.reciprocal(oml_s, oml_s), reads=["lbsum"], writes=["lbrs"])
        tt(lb_s, lbt3[:, 0, :], oml_s, ALU.mult, ["lbe", "lbrs"], ["lb"])
        ts(oml_s, lb_s, -1.0, 1.0, ALU.mult, ALU.add, ["lb", "lbrs"], ["oml"])
        ts(noml_s, oml_s, -1.0, None, ALU.mult, ALU.bypass, ["oml"], ["noml"])
        P.barrier()
        CONST = ["ident", "cmask", "rstm", "gmix", "gffn", "gfin", "pscale", "hnorm", "rmask", "ones", "lb", "oml", "noml"]

        def norm_stats(chunks, ncols, halo):
            n = len(chunks)
            pk, pt = next_ps()
            hk, hbank = next_ps() if halo else (None, None)
            for i, getter in enumerate(chunks):
                key, ap = getter()
                s = i % 4
                act(sqb[s][:, 0:ncols], ap, AF.Square, [key], [("sq", s)])
                if halo:
                    def fn(pe, s=s, i=i):
                        pe.matmul(hbank[:, 0:HALO], ones, sqb[s][:, 0:HALO], start=(i == 0), stop=(i == n - 1))
                        return pe.matmul(pt[:, 0:TT], ones, sqb[s][:, HALO:HALO + TT], start=(i == 0), stop=(i == n - 1))
                    P.op("pe", fn, reads=[("sq", s), "ones"], writes=[pk, hk])
                else:
                    def fn(pe, s=s, i=i):
                        return pe.matmul(pt[:, 0:ncols], ones, sqb[s][:, 0:ncols], start=(i == 0), stop=(i == n - 1))
                    P.op("pe", fn, reads=[("sq", s), "ones"], writes=[pk])
            inv = 1.0 / (128.0 * n)
            if halo:
                act(rstd_b[:, 0:HALO], hbank[:, 0:HALO], AF.Sqrt, [hk], ["rstd"], scale=inv, bias=EPS)
                act(rstd_b[:, HALO:HALO + TT], pt[:, 0:TT], AF.Sqrt, [pk], ["rstd"], scale=inv, bias=EPS)
                P.op("dve", lambda e: e.reciprocal(rstd_b[:, 0:W], rstd_b[:, 0:W]), reads=["rstd"], writes=["rstd"])
            else:
                act(rstd_b[:, 0:ncols], pt[:, 0:ncols], AF.Sqrt, [pk], ["rstd"], scale=inv, bias=EPS)
                P.op("dve", lambda e: e.reciprocal(rstd_b[:, 0:ncols], rstd_b[:, 0:ncols]), reads=["rstd"], writes=["rstd"])

        def make_u(uT, j, halo):
            col0 = j * TT if halo else j * TT + HALO
            ncols = W if halo else TT
            norm_stats([(lambda kc=kc: xload(kc * 128, col0, ncols, alt=True)) for kc in range(KC)], ncols, halo)
            for kc in range(KC):
                key, ap = xload(kc * 128, col0, ncols, alt=True)
                dst = uT[:, kc, 0:ncols] if halo else uT[:, kc, HALO:HALO + TT]
                stt(dst, ap, gmix_s[:, kc:kc + 1], rstd_b[:, 0:ncols], ALU.mult, ALU.mult,
                    [key, "rstd", "gmix"], [("u", kc)])

        A.reset(base_mark)
        olocal = v3(A.take(H * TC), H)
        qcore = v3(A.take(H * TC), H)
        S_all = v3(A.take(H * 128, F32), H)
        BcC = A.take(H, F32)
        Dcore = A.take(H, F32)
        hmark = A.mark()
        uT = v3(A.take(KC * W), KC)
        hb = []
        for par in range(2):
            d = {}
            for nm in ("vT", "qt", "kt", "qh", "kd"):
                d[nm] = A.take(TT)
            d["dE"] = A.take(NCH, F32)
            hb.append(d)
        cb = []
        for par in range(2):
            cb.append({"vtok": A.take(128), "kdT": A.take(128), "pT": A.take(CH)})
        tmpf = {nm: A.take(TT, F32) for nm in ("sg", "q", "sf", "f", "k", "g", "b", "arg", "e")}
        bend = A.take(NCH, F32)
        bmid = A.take(NCH, F32)
        bci = A.take(NCH, F32)
        bce = A.take(NCH, F32)
        onesn = A.take(NCH, F32)
        Sbf_one = A.take(128)
        print("arena phase H cols:", A.off, "of", NB)

        P.op("dve", lambda e: e.memset(S_all.rearrange("p a b -> p (a b)"), 0.0), writes=["S"])
        P.op("dve", lambda e: e.memset(BcC, 0.0), writes=["BcC"])
        P.op("dve", lambda e: e.memset(onesn, 1.0), writes=["onesn"])

        def issue_proj(j, h):
            res = {}
            for nm, off in (("q", c.OFF_Q), ("f", c.OFF_F), ("i", c.OFF_I)):
                wk, wt = wload(wview(w_in, 0, D, off + h * 128, 128), KC, 128)
                pk, pt = next_ps()
                mm(pk, pt[:, 0:TT], [(wt[:, kc, :], uT[:, kc, HALO:HALO + TT]) for kc in range(KC)],
                   [wk] + [("u", kc) for kc in range(KC)])
                res[nm] = (pk, pt[:, 0:TT])
            return res

        def issue_prep(j, h, pr):
            par = h % 2
            B = hb[par]
            t = {k: v[:, 0:TT] for k, v in tmpf.items()}
            pq, pf, pi = pr["q"], pr["f"], pr["i"]
            K = lambda nm: (nm, par)
            act(t["sg"], pq[1], AF.Sigmoid, [pq[0]], ["t_sg"])
            tt(t["q"], pq[1], t["sg"], ALU.mult, [pq[0], "t_sg"], ["t_q"])
            act(t["sf"], pf[1], AF.Sigmoid, [pf[0]], ["t_sf"])
            ts(t["f"], t["sf"], oml_s[:, h:h + 1], lb_s[:, h:h + 1], ALU.mult, ALU.add, ["t_sf", "oml", "lb"], ["t_f"])
            ts(t["k"], t["sf"], noml_s[:, h:h + 1], oml_s[:, h:h + 1], ALU.mult, ALU.add, ["t_sf", "oml", "noml"], ["t_k"])
            act(t["g"], t["f"], AF.Ln, ["t_f"], ["t_g"])
            P.op("dve", lambda e: e.tensor_tensor_scan(t["b"], rstm[:, 0:TT], t["g"], 0.0, ALU.mult, ALU.add),
                 reads=["t_g", "rstm"], writes=["t_b"])
            cp(B["vT"], pi[1], [pi[0]], [K("vT")], eng="act")
            b3 = v3(t["b"], NCH)
            arg3 = v3(t["arg"], NCH)
            e3 = v3(t["e"], NCH)
            cp(bend, b3[:, :, CH - 1], ["t_b"], ["bend"])
            cp(bmid, b3[:, :, CH // 2 - 1], ["t_b"], ["bmid"])
            bc = lambda ap: ap.rearrange("p (c o) -> p c o", o=1).broadcast_to([128, NCH, CH])
            tt(arg3, b3, bc(bmid), ALU.subtract, ["t_b", "bmid"], ["t_arg"])
            act(t["e"], t["arg"], AF.Exp, ["t_arg"], ["t_e"])
            tt(B["qt"], t["q"], t["e"], ALU.mult, ["t_q", "t_e"], [K("qt")])
            act(t["e"], t["arg"], AF.Exp, ["t_arg", K("qt")], ["t_e"], scale=-1.0)
            tt(B["kt"], t["k"], t["e"], ALU.mult, ["t_k", "t_e"], [K("kt")])
            act(t["e"], t["b"], AF.Exp, ["t_b", K("kt")], ["t_e"])
            tt(B["qh"], t["q"], t["e"], ALU.mult, ["t_q", "t_e"], [K("qh")])
            tt(arg3, b3, bc(bend), ALU.subtract, ["t_b", "bend", K("kt")], ["t_arg"])
            act(t["e"], t["arg"], AF.Exp, ["t_arg", K("qh")], ["t_e"], scale=-1.0)
            tt(B["kd"], t["k"], t["e"], ALU.mult, ["t_k", "t_e"], [K("kd")])
            act(B["dE"], bend, AF.Exp, ["bend"], [K("dE")])
            P.op("dve", lambda e: e.tensor_tensor_scan(bci, onesn, bend, BcC[:, h:h + 1], ALU.mult, ALU.add),
                 reads=["bend", "onesn", "BcC"], writes=["bci"])
            tt(bce, bci, bend, ALU.subtract, ["bci", "bend"], ["bce"])
            cp(BcC[:, h:h + 1], bci[:, NCH - 1:NCH], ["bci"], ["BcC"])
            tt(arg3, b3, bc(bce), ALU.add, ["t_b", "bce", K("kd")], ["t_arg"])
            act(t["e"], t["arg"], AF.Exp, ["t_arg", K("kd")], ["t_e"])
            tt(qcore[:, h, j * TT:(j + 1) * TT], t["q"], t["e"], ALU.mult, ["t_q", "t_e"], [("qcore", h)])

        def issue_rec(j, h):
            par = h % 2
            B = hb[par]
            K = lambda nm: (nm, par)
            for cc in range(NCH):
                cp_ = cc % 2
                C = cb[cp_]
                sl = slice(cc * CH, (cc + 1) * CH)
                bTR, bSO, bSU = 3 + cp_, 5 + cp_, 7
                tvp = psb[bTR][0:CH, 0:64].bitcast(BF16)
                tkp = psb[bTR][0:CH, 64:128].bitcast(BF16)
                scp = psb[bSO][0:CH, 0:64]
                op_ = psb[bSO][:, 64:128]
                sp_ = psb[bSU][:, 64:192]
                kk = lambda nm: (nm, cp_)

                def ftr(pe, tvp=tvp, tkp=tkp, sl=sl, B=B):
                    pe.transpose(tvp, B["vT"][:, sl], ident)
                    return pe.transpose(tkp, B["kd"][:, sl], ident)
                P.op("pe", ftr, reads=[K("vT"), K("kd"), "ident"], writes=[("ps", bTR)])
                cp(C["vtok"][0:CH, :], tvp, [("ps", bTR)], [kk("vtok")], eng="act")
                cp(C["kdT"][0:CH, :], tkp, [("ps", bTR)], [kk("kdT")], eng="act")
                mm(("ps", bSO), scp, [(B["kt"][:, sl], B["qt"][:, sl])], [K("kt"), K("qt")])
                tt(C["pT"][0:CH, :], scp, cmask[0:CH, :], ALU.mult, [("ps", bSO), "cmask"], [kk("pT")])
                mm(("ps", bSO), op_, [(C["vtok"][0:CH, :], C["pT"][0:CH, :]), (Sbf_cur[0], B["qh"][:, sl])],
                   [kk("vtok"), kk("pT"), "Sbf", K("qh")])
                cp(olocal[:, h, j * TT + cc * CH:j * TT + (cc + 1) * CH], op_, [("ps", bSO)], [("olocal", h)])
                mm(("ps", bSU), sp_, [(C["kdT"][0:CH, :], C["vtok"][0:CH, :])], [kk("kdT"), kk("vtok")])
                stt(S_all[:, h, :], S_all[:, h, :], B["dE"][:, cc:cc + 1], sp_, ALU.mult, ALU.add,
                    [("ps", bSU), K("dE"), "S"], ["S"])
                cp(Sbf_cur[0], S_all[:, h, :], ["S"], ["Sbf"], eng="act")

        Sbf_cur = [Sbf_one]

        for j in range(NT):
            make_u(uT, j, halo=False)
            prs = {0: issue_proj(j, 0)}
            for h in range(H):
                cp(Sbf_cur[0], S_all[:, h, :], ["S"], ["Sbf"], eng="act")
                issue_prep(j, h, prs[h])
                if h + 1 < H:
                    prs[h + 1] = issue_proj(j, h + 1)
                issue_rec(j, h)

        P.barrier()
        if STOP == 'R':
            raise_stop = True
        A.reset(hmark)
        Sin = v3(A.take(H * 128, F32), H)
        Sinbf = v3(A.take(H * 128), H)
        rbuf = [A.take(SW, F32) for _ in range(2)]
        deff = A.take(H, F32)
        ot = [A.take(512, F32) for _ in range(4)]
        rstds = [A.take(W, F32) for _ in range(4)]
        print("arena exchange cols:", A.off, "of", NB)
        act(Dcore, BcC, AF.Exp, ["BcC"], ["Dcore"])
        P.dma("sp", lambda g: g.dma_start(out=cc_src[:, 0:H * 128], in_=S_all.rearrange("p a b -> p (a b)")),
              reads=["S"], writes=["ccsrc"], semname="d:m")
        P.dma("sp", lambda g: g.dma_start(out=cc_src[:, H * 128:SW], in_=Dcore),
              reads=["Dcore"], writes=["ccsrc2"], semname="d:m")
        P.dma("pool", lambda g: g.collective_compute("AllGather", ALU.bypass, replica_groups=[list(range(c.NCORES))],
                                                      ins=[cc_src], outs=[cc_dst]),
              reads=["ccsrc", "ccsrc2"], writes=["ccdst"], semname="d:cc", inc=1)
        P.op("dve", lambda e: e.memset(Sin.rearrange("p a b -> p (a b)"), 0.0), writes=["Sin"])
        for r in range(c.NCORES):
            s = r % 2
            P.dma("sp", lambda g, r=r, s=s: g.dma_start(out=rbuf[s], in_=cc_dst[r * 128:(r + 1) * 128, :]),
                  reads=["ccdst"], writes=[("rbuf", s)], semname=f"d:r{s}")
            m = rmask_s[:, r:r + 1]
            m1 = rmask_s[:, c.NCORES + r:c.NCORES + r + 1]
            Dr = rbuf[s][:, H * 128:SW]
            Sr = v3(rbuf[s][:, 0:H * 128], H)
            ts(deff, Dr, m, m1, ALU.mult, ALU.add, [("rbuf", s), "rmask"], ["deff"])
            tt(Sin, Sin, deff.rearrange("p (a o) -> p a o", o=1).broadcast_to([128, H, 128]), ALU.mult, ["Sin", "deff"], ["Sin"])
            stt(Sin, Sr, m, Sin, ALU.mult, ALU.add, [("rbuf", s), "rmask", "Sin"], ["Sin"])
        cp(Sinbf, Sin, ["Sin"], ["Sinbf"])
        NHALF = TC // TT
        HN = TT
        state["nrot"] = 8
        items = [(h, hf) for h in range(H) for hf in range(NHALF)]
        for b0 in range(0, len(items), 4):
            blk = items[b0:b0 + 4]
            pks = []
            for s_, (h, hf) in enumerate(blk):
                sl = slice(hf * HN, (hf + 1) * HN)
                pk, pt = next_ps()
                mm(pk, pt[:, 0:HN], [(Sinbf[:, h, :], qcore[:, h, sl])], ["Sinbf", ("qcore", h)])
                pks.append((pk, pt))
            for s_, (h, hf) in enumerate(blk):
                sl = slice(hf * HN, (hf + 1) * HN)
                pk, pt = pks[s_]
                tt(ot[s_][:, 0:HN], pt[:, 0:HN], olocal[:, h, sl], ALU.add, [pk, ("olocal", h)], [("ot", s_)])
                act(sqb[s_][:, 0:HN], ot[s_][:, 0:HN], AF.Square, [("ot", s_)], [("sq", s_)])
            pk2s = []
            for s_, (h, hf) in enumerate(blk):
                pk2, pt2 = next_ps()
                mm(pk2, pt2[:, 0:HN], [(ones, sqb[s_][:, 0:HN])], [("sq", s_), "ones"])
                pk2s.append((pk2, pt2))
            for s_, (h, hf) in enumerate(blk):
                pk2, pt2 = pk2s[s_]
                rb = rstds[s_]
                rk = ("rstdx", s_)
                act(rb[:, 0:HN], pt2[:, 0:HN], AF.Sqrt, [pk2], [rk], scale=1.0 / 128.0, bias=EPS)
            for s_, (h, hf) in enumerate(blk):
                rb = rstds[s_]
                rk = ("rstdx", s_)
                P.op("dve", lambda e, rb=rb: e.reciprocal(rb[:, 0:HN], rb[:, 0:HN]), reads=[rk], writes=[rk])
                tt(ot[s_][:, 0:HN], ot[s_][:, 0:HN], rb[:, 0:HN], ALU.mult, [("ot", s_), rk], [("ot", s_)], eng="pool")
                act(sqb[s_][:, 0:HN], ot[s_][:, 0:HN], AF.Copy, [("ot", s_), "hnorm", ("sq", s_)], [("sq", s_)],
                    scale=hnorm_s[:, h:h + 1])
            for s_, (h, hf) in enumerate(blk):
                sl = slice(hf * HN, (hf + 1) * HN)
                P.dma("sp", lambda g, s_=s_, h=h, sl=sl: g.dma_start(out=onorm_d[h * 128:(h + 1) * 128, sl], in_=sqb[s_][:, 0:HN]),
                      reads=[("sq", s_)], writes=["onorm_d"], semname="d:m")
        P.barrier()

        state["nrot"] = 8
        A.reset(base_mark)
        regA = A.take(KC * W + 2 * (KC // 2) * TT + 64)
        uT = v3(regA[:, 0:KC * W], KC)
        ypool = v3(regA[:, KC * W:KC * W + (KC // 2) * TT], KC // 2)
        yhgrn = v3(regA[:, KC * W + (KC // 2) * TT:KC * W + 2 * (KC // 2) * TT], KC // 2)
        acc = v3(regA[:, 0:2 * KC * TT].bitcast(F32), KC)
        assert 2 * KC * TT <= KC * W + 2 * (KC // 2) * TT + 64
        regB = A.take(KC * TT)
        merged = v3(regB, KC)
        vT = merged
        FB = 8
        scr_bf = A.take(2 * FB * TT)
        hid = [v3(scr_bf[:, i * FB * TT:(i + 1) * FB * TT], FB) for i in range(2)]
        pooledT = [v3(scr_bf[:, i * FB * TT:i * FB * TT + c.PGC * TT], c.PGC) for i in range(2)]
        onh = [A.take(TT) for _ in range(2)]
        NTF = 8
        tf = [A.take(W, F32) for _ in range(NTF)]
        ostg = [A.take(TT, F32) for _ in range(2)]
        pinv_s = A.take(4 * HALO, F32)
        zh = v3(A.take(4 * c.PGC * HALO, F32), 4 * c.PGC)
        print("arena phase 2 cols:", A.off, "of", NB)

        def proj_super(wd, kcn, col0, nn, rhs, rkeys, N=TT, halo_rhs=None):
            banks = [next_ps() for _ in range(nn)]
            hbanks = [next_ps() for _ in range(nn)] if halo_rhs is not None else []
            for k0 in range(0, kcn, 8):
                nk = min(8, kcn - k0)
                wk, wt = wload(wview(wd, k0 * 128, nk * 128, col0, nn * 128), nk, nn * 128)

                def fn(pe, wt=wt, k0=k0, nk=nk):
                    ins = None
                    for n in range(nn):
                        for k in range(nk):
                            if hbanks:
                                pe.matmul(hbanks[n][1][:, 0:HALO], wt[:, k, n * 128:(n + 1) * 128], halo_rhs(k0 + k),
                                          start=(k0 + k == 0), stop=(k0 + k == kcn - 1))
                            ins = pe.matmul(banks[n][1][:, 0:N], wt[:, k, n * 128:(n + 1) * 128], rhs(k0 + k),
                                            start=(k0 + k == 0), stop=(k0 + k == kcn - 1))
                    return ins
                P.op("pe", fn, reads=[wk] + rkeys, writes=[b[0] for b in banks] + [b[0] for b in hbanks])
            if hbanks:
                return [(b[0], b[1][:, 0:N]) for b in banks], [(b[0], b[1][:, 0:HALO]) for b in hbanks]
            return [(b[0], b[1][:, 0:N]) for b in banks]

        for j in range(NT if STOP != 'H' else 0):
            t0 = j * TT
            make_u(uT, j, halo=True)
            P.dma("sp", lambda g, j=j: g.dma_start(out=pinv_s, in_=pinv_d[j * 128:(j + 1) * 128, :]),
                  reads=(), writes=["pinv"], semname="d:pv")
            ukeys = [("u", kc) for kc in range(KC)]
            u_main = lambda k: uT[:, k, HALO:W]
            for g in range(4):
                wsz = 2 << g
                pp = pooledT[g % 2]
                if j == 0:
                    bk, hbk = proj_super(w_in, KC, g * c.PG, c.PGC, u_main, ukeys, halo_rhs=lambda k: uT[:, k, 0:HALO])
                    for cc in range(c.PGC):
                        cp(zh[:, g * c.PGC + cc, :], hbk[cc][1], [hbk[cc][0]], [("zh", g * c.PGC + cc)], eng="act")
                else:
                    bk = proj_super(w_in, KC, g * c.PG, c.PGC, u_main, ukeys)
                for cc in range(c.PGC):
                    idx = g * c.PGC + cc
                    pk, pt = bk[cc]
                    z = tf[0]
                    cp(z[:, 0:HALO], zh[:, idx, :], [("zh", idx)], [("tf", 0)], eng="act")
                    cp(z[:, HALO:W], pt, [pk, ("tf", 0)], [("tf", 0)], eng="act")
                    cp(zh[:, idx, :], z[:, TT:W], [("tf", 0)], [("zh", idx)], eng="act")
                    cur = z
                    ckeys = [("tf", 0)]
                    sh = 1
                    for st in range(g + 1):
                        nxt = tf[1 + st % 2]
                        nk_ = ("tf", 1 + st % 2)
                        tt(nxt[:, sh:W], cur[:, sh:W], cur[:, 0:W - sh], ALU.add, ckeys, [nk_])
                        cur = nxt
                        ckeys = [nk_]
                        sh *= 2
                    pkey = ("scr", g % 2, cc)
                    stt(pp[:, cc, :], cur[:, HALO:W], 1.0 / wsz, z[:, HALO:W], ALU.mult, ALU.subtract,
                        ckeys + [("tf", 0)], [pkey])
                    tt(tf[3][:, 0:HALO], cur[:, HALO:2 * HALO], pinv_s[:, g * HALO:(g + 1) * HALO], ALU.mult,
                       ckeys + ["pinv"], [("tf", 3)])
                    tt(pp[:, cc, 0:HALO], tf[3][:, 0:HALO], z[:, HALO:2 * HALO], ALU.subtract, [("tf", 3), ("tf", 0), pkey], [pkey])
                wk, wt = wload(wview(w_pool, g * c.PG, c.PG, 0, c.PG), c.PGC, c.PG)
                for e in range(c.PGC):
                    pk, pt = next_ps()
                    mm(pk, pt[:, 0:TT], [(wt[:, cc, e * 128:(e + 1) * 128], pp[:, cc, :]) for cc in range(c.PGC)],
                       [wk] + [("scr", g % 2, cc) for cc in range(c.PGC)])
                    idx = g * c.PGC + e
                    ts(ypool[:, idx, :], pt[:, 0:TT], pscale_s[:, idx:idx + 1], None, ALU.mult, ALU.bypass,
                       [pk, "pscale"], [("ypool", idx)])
            for hg in range(0, H, 4):
                nn = min(4, H - hg)
                bk = proj_super(w_in, KC, c.OFF_OG + hg * 128, nn, u_main, ukeys)
                for i in range(nn):
                    h = hg + i
                    s = h % 2
                    P.dma("sp", lambda g_, s=s, h=h, t0=t0: g_.dma_start(out=onh[s], in_=onorm_d[h * 128:(h + 1) * 128, t0:t0 + TT]),
                          reads=["onorm_d"], writes=[("onh", s)], semname=f"d:on{s}")
                    pk, pt = bk[i]
                    sg = tf[4 + s][:, 0:TT]
                    act(sg, pt, AF.Sigmoid, [pk], [("tf", 4 + s)])
                    tt(sg, pt, sg, ALU.mult, [pk, ("tf", 4 + s)], [("tf", 4 + s)])
                    tt(yhgrn[:, h, :], sg, onh[s], ALU.mult, [("tf", 4 + s), ("onh", s)], [("yhgrn", h)])
            if DEBUG:
                for i in range(KC // 2):
                    P.dma("sp", lambda g_, i=i, t0=t0: g_.dma_start(out=dbg[i * 128:(i + 1) * 128, t0:t0 + TT], in_=ypool[:, i, :]),
                          reads=[("ypool", i)], writes=["dbg"], semname="d:m")
                    P.dma("sp", lambda g_, i=i, t0=t0: g_.dma_start(out=dbg[(KC // 2 + i) * 128:(KC // 2 + i + 1) * 128, t0:t0 + TT], in_=yhgrn[:, i, :]),
                          reads=[("yhgrn", i)], writes=["dbg"], semname="d:m")
            ypk = [("ypool", i) for i in range(KC // 2)]
            yhk = [("yhgrn", i) for i in range(KC // 2)]
            for ng in range(0, KC, 4):
                nn = min(4, KC - ng)
                bA = proj_super(w_in, KC, c.OFF_GA + ng * 128, nn, u_main, ukeys)
                for i in range(nn):
                    act(tf[i][:, 0:TT], bA[i][1], AF.Sigmoid, [bA[i][0]], [("tf", i)])
                bB = proj_super(w_in, KC, c.OFF_GB + ng * 128, nn, u_main, ukeys)
                for i in range(nn):
                    act(tf[4 + i][:, 0:TT], bB[i][1], AF.Sigmoid, [bB[i][0]], [("tf", 4 + i)])
                bP = proj_super(w_up_pool, KC // 2, ng * 128, nn, lambda k: ypool[:, k, :], ypk)
                for i in range(nn):
                    tt(tf[i][:, 0:TT], tf[i][:, 0:TT], bP[i][1], ALU.mult, [("tf", i), bP[i][0]], [("tf", i)])
                bH = proj_super(w_up_hgrn, KC // 2, ng * 128, nn, lambda k: yhgrn[:, k, :], yhk)
                for i in range(nn):
                    tt(tf[4 + i][:, 0:TT], tf[4 + i][:, 0:TT], bH[i][1], ALU.mult, [("tf", 4 + i), bH[i][0]], [("tf", 4 + i)])
                    tt(merged[:, ng + i, :], tf[i][:, 0:TT], tf[4 + i][:, 0:TT], ALU.add, [("tf", i), ("tf", 4 + i)], [("merged", ng + i)])
            P.barrier(soft=True)
            mk = [("merged", i) for i in range(KC)]
            for ng in range(0, KC, 4):
                nn = min(4, KC - ng)
                bk = proj_super(w_out, KC, ng * 128, nn, lambda k: merged[:, k, :], mk)
                for i in range(nn):
                    n = ng + i
                    xk, xa = xload(n * 128, HALO + t0, TT)
                    tt(acc[:, n, :], bk[i][1], xa, ALU.add, [bk[i][0], xk], [("acc", n)])
            norm_stats([(lambda n=n: (("acc", n), acc[:, n, :])) for n in range(KC)], TT, False)
            P.barrier(soft=True)
            for n in range(KC):
                stt(vT[:, n, :], acc[:, n, :], gffn_s[:, n:n + 1], rstd_b[:, 0:TT], ALU.mult, ALU.mult,
                    [("acc", n), "rstd", "gffn"], [("v", n)])
            vk = [("v", i) for i in range(KC)]
            v_rhs = lambda k: vT[:, k, :]
            nfb = -(-c.FC // FB)
            for fb in range(nfb):
                f0 = fb * FB
                nf = min(FB, c.FC - f0)
                hd = hid[fb % 2]
                for q0 in range(0, nf, 4):
                    nn = min(4, nf - q0)
                    bG = proj_super(w_gate, KC, (f0 + q0) * 128, nn, v_rhs, vk)
                    for i in range(nn):
                        sg = tf[i][:, 0:TT]
                        act(sg, bG[i][1], AF.Sigmoid, [bG[i][0]], [("tf", i)])
                        tt(sg, sg, bG[i][1], ALU.mult, [("tf", i), bG[i][0]], [("tf", i)])
                    bU = proj_super(w_up, KC, (f0 + q0) * 128, nn, v_rhs, vk)
                    for i in range(nn):
                        tt(hd[:, q0 + i, :], tf[i][:, 0:TT], bU[i][1], ALU.mult, [("tf", i), bU[i][0]], [("scr", fb % 2, q0 + i)])
                hk_ = [("scr", fb % 2, fi) for fi in range(nf)]
                NB_ = min(512, D)
                for nb in range(D // NB_):
                    wk, wt = wload(wview(w_down, f0 * 128, nf * 128, nb * NB_, NB_), nf, NB_)
                    for ni in range(NB_ // 128):
                        n = nb * (NB_ // 128) + ni
                        pk, pt = next_ps()
                        mm(pk, pt[:, 0:TT], [(wt[:, fi, ni * 128:(ni + 1) * 128], hd[:, fi, :]) for fi in range(nf)], [wk] + hk_)
                        tt(acc[:, n, :], acc[:, n, :], pt[:, 0:TT], ALU.add, [pk, ("acc", n)], [("acc", n)])
            norm_stats([(lambda n=n: (("acc", n), acc[:, n, :])) for n in range(KC)], TT, False)
            for n in range(KC):
                s = n % 2
                stt(ostg[s], acc[:, n, :], gfin_s[:, n:n + 1], rstd_b[:, 0:TT], ALU.mult, ALU.mult,
                    [("acc", n), "rstd", "gfin"], [("ostg", s)])
                P.dma("sp", lambda g_, s=s, n=n, t0=t0: g_.dma_start(out=outT[n * 128:(n + 1) * 128, t0:t0 + TT], in_=ostg[s]),
                      reads=[("ostg", s)], writes=["out"], semname=f"d:o{s}")
            P.barrier()
        P.final_wait("sp", ["d:o0", "d:o1"])

        @block.tensor
        def _(e):
            for f in P.q["pe"]:
                f(e)

        @block.scalar
        def _(e):
            for f in P.q["act"]:
                f(e)

        @block.vector
        def _(e):
            for f in P.q["dve"]:
                f(e)

        @block.gpsimd
        def _(e):
            for f in P.q["pool"]:
                f(e)

        @block.sync
        def _(e):
            for f in P.q["sp"]:
                f(e)
    return nc


def host_inputs(cfg, inp):
    c = cfg
    x = np.asarray(inp["x"], np.float32)[0]
    xTf = np.ascontiguousarray(x.T)
    xTp = np.concatenate([np.zeros((c.D, HALO), np.float32), xTf], axis=1)

    def fm(v, n):
        return np.ascontiguousarray(np.asarray(v, np.float32).reshape(n, 128).T)

    shared = {
        "w_in": np.ascontiguousarray(np.asarray(inp["w_in"], np.float32)[0]),
        "w_pool": np.ascontiguousarray(np.asarray(inp["w_pool_group"], np.float32)[0].reshape(4 * c.PG, c.PG)),
        "w_up_pool": np.ascontiguousarray(np.asarray(inp["w_up_pool"], np.float32)[0]),
        "w_up_hgrn": np.ascontiguousarray(np.asarray(inp["w_up_hgrn"], np.float32)[0]),
        "w_out": np.ascontiguousarray(np.asarray(inp["w_out"], np.float32)[0]),
        "w_gate": np.ascontiguousarray(np.asarray(inp["w_ffn_gate"], np.float32)[0]),
        "w_up": np.ascontiguousarray(np.asarray(inp["w_ffn_up"], np.float32)[0]),
        "w_down": np.ascontiguousarray(np.asarray(inp["w_ffn_down"], np.float32)[0]),
        "g_mix": fm(inp["g_mix"][0], c.KC),
        "g_ffn": fm(inp["g_ffn"][0], c.KC),
        "g_final": fm(inp["g_final"], c.KC),
        "pool_scale": fm(inp["pool_scale"][0], 4 * c.PGC),
        "hgrn_norm": fm(inp["hgrn_norm"][0], c.H),
        "lbp": np.ascontiguousarray(np.concatenate([fm(inp["lb_param"][0], c.H), fm(inp["lb_param"][1], c.H)], axis=1)),
        "ident": np.eye(128, dtype=np.float32),
        "cmask": np.triu(np.ones((CH, CH), np.float32)),
        "rstm": np.tile((np.arange(c.TT) % CH != 0).astype(np.float32)[None, :], (128, 1)),
    }
    maps = []
    for core in range(c.NCORES):
        m = dict(shared)
        m["xT"] = np.ascontiguousarray(xTp[:, core * c.TC:core * c.TC + HALO + c.TC])
        rm = (np.arange(c.NCORES) < core).astype(np.float32)
        m["rmask"] = np.tile(np.concatenate([rm, 1.0 - rm])[None, :], (128, 1)).astype(np.float32)
        pinv = np.zeros((c.NT, 128, 4, HALO), np.float32)
        for j in range(c.NT):
            tg = core * c.TC + j * c.TT + np.arange(HALO)
            for g in range(4):
                wsz = 2 << g
                pinv[j, :, g, :] = (1.0 / np.minimum(wsz, tg + 1))[None, :]
        m["pinv"] = pinv.reshape(c.NT * 128, 4 * HALO)
        maps.append(m)
    return maps


_CACHE = {}


def run(cfg, inp, trace=False):
    key = (cfg.D, cfg.SEQ, cfg.NCORES, cfg.TT)
    if key not in _CACHE:
        _CACHE[key] = build_program(cfg)
    nc = _CACHE[key]
    maps = host_inputs(cfg, inp)
    res = run_bass_kernel_spmd(nc, maps, core_ids=list(range(cfg.NCORES)), **({"trace": True} if trace else {}))
    outs = [np.asarray(r["outT"]) for r in res.results]
    if DEBUG:
        run.dbg = [np.asarray(r["dbg"]).astype(np.float32) for r in res.results]
    full = np.concatenate([o.T for o in outs], axis=0)[None]
    return np.ascontiguousarray(full.astype(np.float32)), res


def kernel(**inputs):
    cfg = Cfg()
    out, _ = run(cfg, inputs)
    return out
```
